# Optimizing a Trainium2 kernel written in Bass

```python
import jax, jax.numpy as jnp
from jax import lax
import numpy as np

D_MODEL = 1024
BATCH = 8
SEQ = 2048
DEPTH = 4
DEC_BATCH = 128
DEC_SEQ = 8
PAST_LEN = 16384
PAGE_SIZE = 128

E_MIX = 2 * D_MODEL
HEAD_DIM = 128
W_A = 6 * HEAD_DIM
W_B = 4 * HEAD_DIM
W_C = E_MIX - W_A - W_B
N_HEADS_A = W_A // HEAD_DIM
CHUNK = 128
POOL_WINDOWS = (2, 4, 8, 16)
N_POOL_GROUPS = len(POOL_WINDOWS)
POOL_GW = W_B // N_POOL_GROUPS
POOL_BUF = max(POOL_WINDOWS) - 1
CONV_W = 3
CONV_BUF = CONV_W - 1
EPS = 1e-6
IN_WIDTHS = (W_A, W_A, W_A, W_B, W_B, W_C, W_C, W_C, W_C)
IN_TOTAL = sum(IN_WIDTHS)
IN_SPLITS = tuple(int(s) for s in np.cumsum(IN_WIDTHS)[:-1])

kernel_name = "hymba_style_gmlp_pool_shortconv_decode_step"


def rmsnorm(x, g):
    xf = x.astype(jnp.float32)
    y = xf * lax.rsqrt(jnp.mean(xf * xf, axis=-1, keepdims=True) + EPS) * g.astype(jnp.float32)
    return y.astype(x.dtype)


def chunk_gmlp(u, v, v_g, w_s, b_s):
    bt, L, _ = v.shape
    vn = rmsnorm(v, v_g)
    n_chunks = -(-L // CHUNK)
    pad = n_chunks * CHUNK - L
    vp = jnp.pad(vn, ((0, 0), (0, pad), (0, 0)))
    vp = vp.reshape(bt, n_chunks, CHUNK, N_HEADS_A, HEAD_DIM)
    mask = jnp.tril(jnp.ones((CHUNK, CHUNK), dtype=bool))
    w_m = jnp.where(mask[None], w_s, jnp.zeros_like(w_s))
    s = jnp.einsum('hts,bcshd->bcthd', w_m, vp) + b_s.T[None, None, :, :, None]
    s = s.reshape(bt, n_chunks * CHUNK, W_A)[:, :L]
    return u * s, vn


def pool_mixer(p, buf, start_pos, w_pg, pool_scale):
    bt, L, _ = p.shape
    pp = jnp.concatenate([buf, p], axis=1)
    cs = jnp.cumsum(pp.astype(jnp.float32), axis=1)
    cs0 = jnp.pad(cs, ((0, 0), (1, 0), (0, 0)))
    pos = start_pos + jnp.arange(L, dtype=jnp.int32)
    hi = cs0[:, POOL_BUF + 1:POOL_BUF + 1 + L]
    outs = []
    for gi, w in enumerate(POOL_WINDOWS):
        c0, c1 = gi * POOL_GW, (gi + 1) * POOL_GW
        lo = cs0[:, POOL_BUF + 1 - w:POOL_BUF + 1 - w + L, c0:c1]
        cnt = jnp.minimum(pos + 1, w).astype(jnp.float32)[None, :, None]
        mean = (hi[..., c0:c1] - lo) / cnt
        d = (mean - p[..., c0:c1].astype(jnp.float32)).astype(p.dtype)
        outs.append(jnp.einsum('bld,de->ble', d, w_pg[gi]))
    out = jnp.concatenate(outs, axis=-1) * pool_scale
    return out, pp[:, -POOL_BUF:]


def short_conv(xc, bg, cg, buf, conv_w):
    L = xc.shape[1]
    cx = cg * xc
    cp = jnp.concatenate([buf, cx], axis=1)
    y = conv_w[0] * cp[:, 0:L] + conv_w[1] * cp[:, 1:L + 1] + conv_w[2] * cp[:, 2:L + 2]
    return bg * y, cp[:, -CONV_BUF:]


def trunk_layer(x, pool_buf, conv_buf, start_pos, pre_g, w_in, v_g, w_s, b_s,
                w_pg, pool_scale, conv_w, w_out, post_g):
    h = rmsnorm(x, pre_g)
    proj = jnp.einsum('bld,de->ble', h, w_in)
    u, v, z_a, p, z_b, xc, bg, cg, z_c = jnp.split(proj, IN_SPLITS, axis=-1)
    a_out, vn = chunk_gmlp(u, v, v_g, w_s, b_s)
    b_out, new_pool = pool_mixer(p, pool_buf, start_pos, w_pg, pool_scale)
    c_out, new_conv = short_conv(xc, bg, cg, conv_buf, conv_w)
    mix = jnp.concatenate([a_out * jax.nn.silu(z_a), b_out * jax.nn.silu(z_b),
                           c_out * jax.nn.silu(z_c)], axis=-1)
    out = jnp.einsum('ble,ed->bld', mix, w_out)
    return x + rmsnorm(out, post_g), new_pool, new_conv, vn


def setup_inputs(seed: int = 0) -> dict:
    key = jax.random.key(seed)
    ks = jax.random.split(key, 16)
    f = jnp.float32
    nrm = lambda k, s: jax.random.normal(k, s, dtype=f)
    return {
        "x_prompt": nrm(ks[0], (BATCH, SEQ, D_MODEL)),
        "x_sample": nrm(ks[1], (DEC_BATCH, DEC_SEQ, D_MODEL)),
        "state_pool": nrm(ks[2], (DEPTH, DEC_BATCH, POOL_BUF, W_B)),
        "state_conv": nrm(ks[3], (DEPTH, DEC_BATCH, CONV_BUF, W_C)) * 0.5,
        "pre_norm_g": 1.0 + 0.05 * nrm(ks[4], (DEPTH, D_MODEL)),
        "w_in": nrm(ks[5], (DEPTH, D_MODEL, IN_TOTAL)) * D_MODEL ** -0.5,
        "v_norm_g": 1.0 + 0.05 * nrm(ks[6], (DEPTH, W_A)),
        "w_spatial": nrm(ks[7], (DEPTH, N_HEADS_A, CHUNK, CHUNK)) * CHUNK ** -0.5,
        "b_spatial": 1.0 + 0.1 * nrm(ks[8], (DEPTH, N_HEADS_A, CHUNK)),
        "w_pool_group": nrm(ks[9], (DEPTH, N_POOL_GROUPS, POOL_GW, POOL_GW)) * POOL_GW ** -0.5,
        "pool_scale": 1.0 + 0.1 * nrm(ks[10], (DEPTH, W_B)),
        "conv_w": nrm(ks[11], (DEPTH, CONV_W, W_C)) * CONV_W ** -0.5,
        "w_out": nrm(ks[12], (DEPTH, E_MIX, D_MODEL)) * E_MIX ** -0.5,
        "post_norm_g": 1.0 + 0.05 * nrm(ks[13], (DEPTH, D_MODEL)),
    }


def reference(x_prompt, x_sample, state_pool, state_conv, pre_norm_g, w_in, v_norm_g,
              w_spatial, b_spatial, w_pool_group, pool_scale, conv_w, w_out, post_norm_g):
    hp = x_prompt
    hs = x_sample
    pool_p, conv_p, pool_s, conv_s, v_s = [], [], [], [], []
    zero_pool = jnp.zeros((x_prompt.shape[0], POOL_BUF, W_B), x_prompt.dtype)
    zero_conv = jnp.zeros((x_prompt.shape[0], CONV_BUF, W_C), x_prompt.dtype)
    for l in range(DEPTH):
        params = (pre_norm_g[l], w_in[l], v_norm_g[l], w_spatial[l], b_spatial[l],
                  w_pool_group[l], pool_scale[l], conv_w[l], w_out[l], post_norm_g[l])
        hp, npool, nconv, _ = trunk_layer(hp, zero_pool, zero_conv, 0, *params)
        pool_p.append(npool)
        conv_p.append(nconv)
        hs, npool, nconv, vn = trunk_layer(hs, state_pool[l], state_conv[l], PAST_LEN, *params)
        pool_s.append(npool)
        conv_s.append(nconv)
        v_s.append(vn)
    return (hp, hs, jnp.stack(pool_p), jnp.stack(conv_p), jnp.stack(pool_s),
            jnp.stack(conv_s), jnp.stack(v_s))
```

```python
import numpy as np
from contextlib import ExitStack

import concourse.bass as bass
import concourse.mybir as mybir
from concourse.bass_utils import run_bass_kernel_spmd

F32 = mybir.dt.float32
BF16 = mybir.dt.bfloat16
AF = mybir.ActivationFunctionType
ALU = mybir.AluOpType

DEPTH = 4
D = 1024
SEQ = 2048
NCORE = 8
SB = 16
ST = 8
WA, WB, WC = 768, 512, 768
EIN = 6400
EPS = 1e-6
OFF_U, OFF_V, OFF_ZA, OFF_P, OFF_ZB, OFF_XC, OFF_BG, OFF_CG, OFF_ZC = (
    0, 768, 1536, 2304, 2816, 3328, 4096, 4864, 5632)
POOL_W = (2, 4, 8, 16)

GROUPS = [
    ([0, 1, 2, 3, 4, 5], [(0, 3, False), (3, 3, False)]),
    ([6, 7, 8, 9, 10, 11], [(0, 3, False), (3, 3, False)]),
    ([12, 13, 14, 15, 16], [(0, 2, False), (2, 3, True)]),
]
GT = 6
GTOK = GT * 128
NSLOT = 14
NSCR = 10

CB_ID = 0
CB_ONES = 1
CB_MFIRST = 2
CB_MCUR = 6
CB_MPREV = 10
CB_MCURS = 14
CB_MHIST = 18
CB_N = 26
C32_ID = 0
C32_MASKP = 1
C32_MASKS = 2
C32_N = 3


def _host_consts():
    t = np.arange(128)
    tp = t[:, None]
    tt = t[None, :]
    eye = np.eye(128)
    cb = np.zeros((128, CB_N, 128), np.float64)
    cb[:, CB_ID] = eye
    cb[0:2, CB_ONES] = 1.0
    sp, jp = tp // ST, tp % ST
    s, j = tt // ST, tt % ST
    for gi, w in enumerate(POOL_W):
        band = ((tp <= tt) & (tp > tt - w)).astype(np.float64)
        cb[:, CB_MCUR + gi] = band / w - eye
        cb[:, CB_MFIRST + gi] = band / np.minimum(tt + 1, w) - eye
        cb[:, CB_MPREV + gi] = ((tp - 128) > (tt - w)).astype(np.float64) / w
        bs = ((sp == s) & (jp <= j) & (jp > j - w)).astype(np.float64)
        cb[:, CB_MCURS + gi] = bs / w - eye
        for half in range(2):
            sl, i = tp // 15, tp % 15
            m = ((tp < 120) & ((8 * half + sl) == s) & ((i - 15) > (j - w)))
            cb[:, CB_MHIST + half * 4 + gi] = m.astype(np.float64) / w
    c32 = np.zeros((128, C32_N, 128), np.float64)
    c32[:, C32_ID] = eye
    c32[:, C32_MASKP] = (tp <= tt)
    c32[:, C32_MASKS] = ((sp == s) & (jp <= j))
    return (np.ascontiguousarray(cb.reshape(128, -1).astype(np.float32)),
            np.ascontiguousarray(c32.reshape(128, -1).astype(np.float32)))


class Res:
    __slots__ = ("w", "r")

    def __init__(self):
        self.w = None
        self.r = {}


class Q:
    def __init__(self, h, sem, is_pe=False):
        self.h = h
        self.sem = sem
        self.count = 0
        self.waited = {}
        self.is_pe = is_pe


class DSem:
    def __init__(self, sem, shared=False):
        self.sem = sem
        self.count = 0
        if shared:
            SHARED[id(sem)] = self


SHARED = {}


def toks_of(rs):
    out = []
    for r in rs:
        if r.w is not None:
            out.append(r.w)
        out.extend(r.r.values())
    return out


def emit(q, fn, reads=(), writes=(), dsem=None, extra=()):
    deps = {}

    def add(tok):
        if tok is None:
            return
        s, v = tok
        k = id(s)
        if k in SHARED:
            v = SHARED[k].count
        if k not in deps or deps[k][1] < v:
            deps[k] = (s, v)

    for r in reads:
        add(r.w)
    for w in writes:
        add(w.w)
        for tok in w.r.values():
            add(tok)
    for tok in extra:
        add(tok)
    for k, (s, v) in deps.items():
        if q.is_pe and s is q.sem:
            continue
        if q.waited.get(k, 0) >= v:
            continue
        q.h.wait_ge(s, v)
        q.waited[k] = v
    ins = fn(q.h)
    if dsem is not None:
        dsem.count += 16
        ins.then_inc(dsem.sem, 16)
        tok = (dsem.sem, dsem.count)
    else:
        q.count += 1
        ins.then_inc(q.sem, 1)
        tok = (q.sem, q.count)
    for w in writes:
        w.w = tok
        w.r = {}
    for r in reads:
        k = id(tok[0])
        if k not in r.r or r.r[k][1] < tok[1]:
            r.r[k] = tok
    return tok


def build_nc():
    nc = bass.Bass("TRN2", target_bir_lowering=False)

    def din(name, shape):
        return nc.dram_tensor(name, list(shape), F32, kind="ExternalInput").ap()

    def dout(name, shape):
        return nc.dram_tensor(name, list(shape), F32, kind="ExternalOutput").ap()

    x_p = din("x_p", [SEQ, D])
    x_s = din("x_s", [128, D])
    sp_in = din("sp_in", [DEPTH, SB * 15, WB])
    sc_in = din("sc_in", [DEPTH, SB * 2, WC])
    pre_g = din("pre_g", [DEPTH, 128, D])
    post_g = din("post_g", [DEPTH, 128, D])
    v_g = din("v_g", [DEPTH, 128, WA])
    w_in = din("w_in", [DEPTH, D, EIN])
    w_sp = din("w_sp", [DEPTH, 6, 128, 128])
    b_sp = din("b_sp", [DEPTH, 2, 768])
    w_pg = din("w_pg", [DEPTH, 4, 128, 128])
    pscale_d = din("pscale", [128, DEPTH * 4])
    convw_d = din("convw", [128, DEPTH * 6 * 3])
    w_out = din("w_out", [DEPTH, 2048, D])
    cstb_d = din("cstb", [128, CB_N * 128])
    cst32_d = din("cst32", [128, C32_N * 128])
    msel_d = din("msel", [128, 2])

    y_p = dout("y_p", [SEQ, D])
    y_s = dout("y_s", [128, D])
    nsp_p = dout("nsp_p", [DEPTH, 15, WB])
    nsc_p = dout("nsc_p", [DEPTH, 2, WC])
    nsp_s = dout("nsp_s", [DEPTH, SB, 15, WB])
    nsc_s = dout("nsc_s", [DEPTH, SB * 2, WC])
    vs_o = dout("vs_o", [DEPTH, 128, WA])

    with ExitStack() as es:
        E = es.enter_context

        def sb(name, shape, dt):
            return E(nc.sbuf_tensor("sb_" + name, list(shape), dt))

        PE = Q(nc.tensor, E(nc.semaphore("s_pe")), is_pe=True)
        ACT = Q(nc.scalar, E(nc.semaphore("s_act")))
        DVE = Q(nc.vector, E(nc.semaphore("s_dve")))
        POOL = Q(nc.gpsimd, E(nc.semaphore("s_pool")))
        SP = Q(nc.sync, E(nc.semaphore("s_sp")))
        all_dsems = []

        def mk_dsem(name, shared=False):
            d = DSem(E(nc.semaphore(name)), shared)
            all_dsems.append(d)
            return d

        ps = E(nc.psum_tensor("ps", [128, 4096], F32))
        bank_res = [Res() for _ in range(8)]

        cstb = sb("cstb", [128, CB_N, 128], BF16)
        c32 = sb("c32", [128, C32_N, 128], F32)
        msel = sb("msel", [128, 2], F32)
        neghalf = sb("neghalf", [128, 1], F32)
        wmt = sb("wmt", [128, DEPTH * 6, 128], BF16)
        bdt = sb("bdt", [128, 6, 128], BF16)
        bdu = sb("bdu", [128, 6, 128], F32)
        sc32 = sb("sc32", [32, WC], F32)
        wpg = sb("wpg", [128, DEPTH * 4, 128], BF16)
        pscale = sb("pscale", [128, DEPTH * 4], F32)
        convw = sb("convw", [128, DEPTH * 6 * 3], F32)
        histc = sb("histc", [128, 6, 32], F32)
        scarry = sb("scarry", [128, DEPTH * 6, 32], F32)
        ccfull = sb("ccarry", [128, 128], F32)
        ccarry = ccfull[:, 0:DEPTH * 12].rearrange("p (a b) -> p a b", b=2)
        phalo = sb("phalo", [128, DEPTH, WB], BF16)
        histp = sb("histp", [120, 2, WB], BF16)
        stats = sb("stats", [128, 8], F32)
        pre_bc = sb("pre_bc", [128, D], F32)
        post_bc = sb("post_bc", [128, D], F32)
        vg_bc = sb("vg_bc", [128, WA], F32)
        bias2 = sb("bias2", [128, 2 * 768], BF16)

        xg = sb("xg", [128, GT, D], F32)
        hbuf = sb("hbuf", [128, 2, D], BF16)
        hT = sb("hT", [128, 8, GTOK], BF16)
        mix = sb("mix", [128, 16, GTOK], BF16)
        woutb = sb("woutb", [128, 16 * D], BF16)
        ring = sb("ring", [128, NSLOT * 1024], BF16)
        scr = sb("scr", [128, NSCR, 512], F32)
        cxb = sb("cxb", [128, 2, 2 + 512], F32)
        stg_v = sb("stg_v", [128, WA], F32)
        stg_p = sb("stg_p", [128, WB], F32)
        R_stg_v, R_stg_p = Res(), Res()
        stg_c = sb("stg_c", [128, 384], F32)
        R_stg_c = Res()
        d_nsc = mk_dsem("d_nsc", True)

        vn = woutb[:, 0:GT * WA].rearrange("p (j c) -> p j c", j=GT)
        ptm = woutb[:, GT * WA:GT * WA + (GT + 1) * WB].rearrange("p (j c) -> p j c", j=GT + 1)
        wout = woutb[:].rearrange("p (k d) -> p k d", k=16)

        R = {n: Res() for n in (
            "cstb", "c32", "msel", "neghalf", "wmt", "bdt", "wpg", "pscale", "convw", "histc",
            "scarry", "ccarry", "histp", "pre_bc", "post_bc", "vg_bc", "bias2", "bdu", "sc32", "cid")}
        R_phalo = [Res() for _ in range(DEPTH)]
        R_x = [Res() for _ in range(GT)]
        R_h = [Res(), Res()]
        R_hT = [Res() for _ in range(GT)]
        R_mix = [[Res() for _ in range(2)] for _ in range(16)]
        R_vn = [Res() for _ in range(GT)]
        R_ptm = [Res() for _ in range(GT + 1)]
        R_scr = [Res() for _ in range(NSCR)]
        R_cxb = [Res(), Res()]
        R_stat = [Res() for _ in range(8)]
        alias_res = R_vn + R_ptm

        d_setup = mk_dsem("d_setup", True)
        d_setup_sw = mk_dsem("d_setup_sw", True)
        d_x = [mk_dsem(f"d_x{j}") for j in range(GT)]
        d_par = {n: mk_dsem("d_" + n) for n in ("pre", "post", "vg", "histp", "sc32")}
        d_brow = mk_dsem("d_brow", True)
        d_bdu = mk_dsem("d_bdu")
        d_cid = mk_dsem("d_cid")
        d_vs = mk_dsem("d_vs")
        d_nsp = mk_dsem("d_nsp", True)
        d_msel = mk_dsem("d_msel")
        d_wout = [mk_dsem(f"d_wout{i}") for i in range(4)]
        R_wout = [Res() for _ in range(4)]
        d_ring = [mk_dsem(f"d_r{i}") for i in range(16)]
        d_out = mk_dsem("d_out", True)
        d_stage = mk_dsem("d_stage", True)

        st_bank = [0]

        def alloc_banks(n=1):
            p = st_bank[0]
            if n == 2 and p % 2 == 1:
                p += 1
            if p + n > 8:
                p = 0
            st_bank[0] = p + n
            return p

        st_scr = [0]

        def alloc_scr(n=1):
            p = st_scr[0]
            if p + n > NSCR:
                p = 0
            st_scr[0] = p + n
            return p

        st_stat = [0]

        def alloc_stat():
            p = st_stat[0]
            st_stat[0] = (p + 1) % 8
            return p

        def bank(b, n=512):
            return ps[:, 512 * b:512 * b + n]

        slot_res = [Res() for _ in range(NSLOT)]
        slot_owner = [None] * NSLOT
        ring_state = {"pos": 0, "n": 0}
        pending = []

        class WAlloc:
            pass

        def ring_try_alloc():
            if not pending:
                return False
            item = pending[0]
            n = item["slots"]
            pos = ring_state["pos"]
            if pos + n > NSLOT:
                if any(slot_owner[s] is not None for s in range(pos, NSLOT)):
                    return False
                pos = 0
            if any(slot_owner[s] is not None for s in range(pos, pos + n)):
                return False
            pending.pop(0)
            a = item["alloc"]
            a.slots = list(range(pos, pos + n))
            a.res = [slot_res[s] for s in a.slots]
            for s in a.slots:
                slot_owner[s] = a
            ring_state["pos"] = pos + n
            idx = ring_state["n"]
            ring_state["n"] += 1
            ds = d_ring[idx % 16]
            prev_tok = (ds.sem, ds.count) if ds.count > 0 else None
            ncols = item["ncols"]
            view = ring[:, pos * 1024:pos * 1024 + 8 * ncols].rearrange("p (k e) -> p k e", k=8)
            a.ap = view
            src = w_in[item["l"]].rearrange("(k p) e -> p k e", p=128)[:, :, item["c0"]:item["c0"] + ncols]
            emit(POOL, lambda g: g.dma_start(out=view, in_=src), writes=a.res, dsem=ds,
                 extra=[prev_tok] if prev_tok else [])
            return True

        def pump():
            while ring_try_alloc():
                pass

        def release(a):
            for s in a.slots:
                slot_owner[s] = None
            pump()

        def need(a):
            guard = 0
            while not hasattr(a, "slots"):
                assert ring_try_alloc(), "weight ring deadlock (increase NSLOT)"
                guard += 1
                assert guard < 100
            return a

        pass_list = [(g, l) for g in range(len(GROUPS)) for l in range(DEPTH)]
        wsched = {}
        for (g, l) in pass_list:
            d = {}

            def mk(name, c0, ncols):
                a = WAlloc()
                pending.append({"alloc": a, "l": l, "c0": c0, "ncols": ncols, "slots": ncols // 128})
                d[name] = a

            mk("v", OFF_V, 768)
            mk("p", OFF_P, 512)
            for gi in range(4):
                mk(("zb", gi), OFF_ZB + gi * 128, 128)
            for h in range(6):
                mk(("za", h), OFF_ZA + h * 128, 128)
                mk(("u", h), OFF_U + h * 128, 128)
            for c in range(6):
                mk(("zc", c), OFF_ZC + c * 128, 128)
                mk(("cg", c), OFF_CG + c * 128, 128)
                mk(("xc", c), OFF_XC + c * 128, 128)
                mk(("bg", c), OFF_BG + c * 128, 128)
            wsched[(g, l)] = d

        def act(out, in_, func, reads, writes, scale=1.0, accum=None):
            kw = {}
            if accum is not None:
                kw["accum_out"] = accum
            return emit(ACT, lambda a: a.activation(out=out, in_=in_, func=func, scale=scale, **kw),
                        reads=reads, writes=writes)

        def dve_tt(out, in0, in1, op, reads, writes):
            return emit(DVE, lambda v: v.tensor_tensor(out=out, in0=in0, in1=in1, op=op),
                        reads=reads, writes=writes)

        def dve_stt(out, in0, scalar, in1, op0, op1, reads, writes):
            return emit(DVE, lambda v: v.scalar_tensor_tensor(out=out, in0=in0, scalar=scalar, in1=in1,
                                                              op0=op0, op1=op1),
                        reads=reads, writes=writes)

        def dve_copy(out, in_, reads, writes):
            return emit(DVE, lambda v: v.tensor_copy(out, in_), reads=reads, writes=writes)

        def rstd_from(si):
            col = stats[:, si:si + 1]
            emit(POOL, lambda g: g.tensor_scalar(out=col, in0=col, scalar1=EPS, scalar2=None, op0=ALU.add),
                 reads=[R_stat[si]], writes=[R_stat[si]])
            emit(POOL, lambda g: g.tensor_tensor(out=col, in0=col, in1=neghalf[:], op=ALU.pow),
                 reads=[R_stat[si], R["neghalf"]], writes=[R_stat[si]])
            return col

        def barrier():
            qs = [PE, ACT, DVE, POOL, SP]
            for q in qs:
                for o in qs:
                    if o is q or o.count == 0:
                        continue
                    if q.waited.get(id(o.sem), 0) < o.count:
                        q.h.wait_ge(o.sem, o.count)
                        q.waited[id(o.sem)] = o.count
                for dsm in all_dsems:
                    if dsm.count and q.waited.get(id(dsm.sem), 0) < dsm.count:
                        q.h.wait_ge(dsm.sem, dsm.count)
                        q.waited[id(dsm.sem)] = dsm.count

        w32 = mix[:, 0:8, :].rearrange("p k t -> p (k t)").bitcast(F32).rearrange("p (a b) -> p a b", b=128)
        R_w32 = [R_mix[k][si_] for k in range(8) for si_ in range(2)]

        def setup_loads():
            emit(SP, lambda s_: s_.dma_start(out=c32[:].rearrange("p a b -> p (a b)"), in_=cst32_d),
                 writes=[R["c32"]], dsem=d_setup)
            emit(SP, lambda s_: s_.dma_start(out=pscale[:], in_=pscale_d), writes=[R["pscale"]], dsem=d_setup)
            emit(SP, lambda s_: s_.dma_start(out=convw[:], in_=convw_d), writes=[R["convw"]], dsem=d_setup)
            emit(SP, lambda s_: s_.dma_start(out=w32, in_=w_sp.rearrange("l h t s -> t (l h) s")),
                 writes=R_w32, dsem=d_setup)
            emit(SP, lambda q: q.dma_start(
                out=nsp_s[:, :, 0:7, :],
                in_=sp_in.rearrange("l (s r) c -> l s r c", r=15)[:, :, 8:15, :]), dsem=d_out)

        def setup_pool():
            emit(POOL, lambda g_: g_.dma_start(out=cstb[:, CB_ID, :], in_=cstb_d[:, 0:128]),
                 writes=[R["cid"]], dsem=d_cid)
            ring_try_alloc()
            ring_try_alloc()
            emit(POOL, lambda g_: g_.memset(neghalf[:], -0.5), writes=[R["neghalf"]])
            emit(DVE, lambda v_: v_.memset(bias2[:], 0.0), writes=[R["bias2"]])

        def setup_pool_late():
            emit(POOL, lambda g_: g_.memset(ccfull[:], 0.0), writes=[R["ccarry"]])
            emit(POOL, lambda g_: g_.memset(bdu[:], 0.0), writes=[R["bdu"]])
            emit(POOL, lambda g_: g_.dma_start(out=cstb[:, 1:, :].rearrange("p a b -> p (a b)"), in_=cstb_d[:, 128:]),
                 writes=[R["cstb"]], dsem=d_setup_sw)
            emit(POOL, lambda g_: g_.dma_start(out=wpg[:], in_=w_pg.rearrange("l g d e -> d (l g) e")),
                 writes=[R["wpg"]], dsem=d_setup_sw)
            pump()

        def setup_wmt():
            for lh in range(DEPTH * 6):
                b = alloc_banks()
                emit(PE, lambda pe: pe.transpose(out=bank(b, 128), in_=w32[:, lh, :], identity=c32[:, C32_ID, :]),
                     reads=R_w32 + [R["c32"]], writes=[bank_res[b]])
                dve_tt(wmt[:, lh, :], bank(b, 128), c32[:, C32_MASKP, :], ALU.mult,
                       reads=[bank_res[b], R["c32"]], writes=[R["wmt"]])

        def load_sample_layer(l):
            wsmall = w_sp[l, :, 0:ST, 0:ST].rearrange("h j i -> j h i")
            ex = toks_of([R["bdu"]])
            for s_ in range(SB):
                emit(SP, lambda q, s_=s_: q.dma_start(out=bdu[ST * s_:ST * s_ + ST, :, ST * s_:ST * s_ + ST],
                                                      in_=wsmall),
                     dsem=d_bdu, extra=ex)
            R["bdu"].w = (d_bdu.sem, d_bdu.count)
            R["bdu"].r = {}
            emit(SP, lambda q: q.dma_start(out=sc32[:], in_=sc_in[l]), writes=[R["sc32"]], dsem=d_par["sc32"])

        def prep_sample_layer(l):
            for (h0, nh) in ((0, 4), (4, 2)):
                b = alloc_banks()

                def trs(pe, b=b, h0=h0, nh=nh):
                    for i in range(nh):
                        ins = pe.transpose(out=ps[:, 512 * b + i * 128:512 * b + (i + 1) * 128], in_=bdu[:, h0 + i, :],
                                           identity=c32[:, C32_ID, :])
                    return ins
                emit(PE, trs, reads=[R["bdu"], R["c32"]], writes=[bank_res[b]])
                dve_tt(bdt[:, h0:h0 + nh, :], bank(b, nh * 128).rearrange("p (n t) -> p n t", n=nh),
                       c32[:, C32_MASKS, :].unsqueeze(1).to_broadcast([128, nh, 128]), ALU.mult,
                       reads=[bank_res[b], R["c32"]], writes=[R["bdt"]])
            b = alloc_banks()

            def trc(pe, b=b):
                for c in range(6):
                    ins = pe.transpose(out=ps[:, 512 * b + c * 32:512 * b + (c + 1) * 32], in_=sc32[:, c * 128:(c + 1) * 128],
                                       identity=c32[0:32, C32_ID, 0:32])
                return ins
            emit(PE, trc, reads=[R["sc32"], R["c32"]], writes=[bank_res[b]])
            act(histc[:].rearrange("p c k -> p (c k)"), bank(b, 192), AF.Copy, reads=[bank_res[b]], writes=[R["histc"]])

        def load_params_early(g, l, defer=False):
            emit(SP, lambda q: q.dma_start(out=pre_bc[:], in_=pre_g[l]), writes=[R["pre_bc"]], dsem=d_par["pre"])
            emit(SP, lambda q: q.dma_start(out=vg_bc[:], in_=v_g[l]), writes=[R["vg_bc"]], dsem=d_par["vg"])
            b3 = alloc_scr(3)
            b2 = alloc_scr(2)
            rb3 = [R_scr[b3 + i] for i in range(3)]
            rb2 = [R_scr[b2 + i] for i in range(2)]
            brow = scr[0:2, b3:b3 + 3, :].rearrange("p a b -> p (a b)")
            bhi = scr[0:2, b2:b2 + 2, :].rearrange("p a b -> p (a b)").bitcast(BF16)[:, 0:1536]
            ex = toks_of(rb3)
            for r_ in range(2):
                emit(SP, lambda q, r_=r_: q.dma_start(out=brow[r_:r_ + 1, :],
                                                      in_=b_sp[l:l + 1].rearrange("o k c -> o (k c)")),
                     dsem=d_brow, extra=ex)
            for r_ in rb3:
                r_.w = (d_brow.sem, d_brow.count)
                r_.r = {}
            if g == 2:
                emit(POOL, lambda q: q.dma_start(out=histp[:], in_=sp_in[l].rearrange("(h r) c -> r h c", h=2)),
                     writes=[R["histp"]], dsem=d_par["histp"])
                load_sample_layer(l)
            def bias_ops():
                dve_copy(bhi, brow, reads=rb3, writes=rb2)
                dve_tt(brow, brow, bhi, ALU.subtract, reads=rb3 + rb2, writes=rb3)
                emit(DVE, lambda v: v.tensor_scalar(out=bhi, in0=bhi, scalar1=msel[0:2, 0:1], scalar2=None,
                                                    op0=ALU.mult),
                     reads=rb2 + [R["msel"]], writes=rb2)
                dve_stt(bias2[0:2, :], brow, msel[0:2, 1:2], bhi, ALU.mult, ALU.add,
                        reads=rb3 + rb2 + [R["msel"]], writes=[R["bias2"]])
            if defer:
                return bias_ops
            bias_ops()

        def load_post(l):
            emit(SP, lambda q: q.dma_start(out=post_bc[:], in_=post_g[l]), writes=[R["post_bc"]], dsem=d_par["post"])

        def wout_piece(l, k, ex=()):
            src = w_out[l].rearrange("(k p) d -> p k d", p=128)
            q4 = k // 4
            ex2 = list(ex) + (toks_of([R_wout[q4]]) if k % 4 == 0 else [])
            emit(POOL, lambda q: q.dma_start(out=wout[:, k, :], in_=src[:, k, :]), dsem=d_wout[q4], extra=ex2)
            if k % 4 == 3:
                R_wout[q4].w = (d_wout[q4].sem, d_wout[q4].count)
                R_wout[q4].r = {}

        def n_ew(g, l, j):
            si = alloc_stat()
            ju = alloc_scr()
            act(scr[:, ju, :].bitcast(BF16), xg[:, j, :], AF.Square, reads=[R_x[j]],
                writes=[R_scr[ju], R_stat[si]], scale=float(D) ** -0.5, accum=stats[:, si:si + 1])
            rs = rstd_from(si)
            hb = j % 2
            dve_stt(hbuf[:, hb, :], xg[:, j, :], rs, pre_bc[:], ALU.mult, ALU.mult,
                    reads=[R_x[j], R_stat[si], R["pre_bc"]], writes=[R_h[hb]])

        def t_job(j):
            hb = j % 2
            b = alloc_banks()
            pb = bank(b).bitcast(BF16)

            def tr(pe):
                for k in range(8):
                    ins = pe.transpose(out=pb[:, k * 128:(k + 1) * 128], in_=hbuf[:, hb, k * 128:(k + 1) * 128],
                                       identity=cstb[:, CB_ID, :])
                return ins
            emit(PE, tr, reads=[R_h[hb], R["cid"]], writes=[bank_res[b]])
            act(hT[:, :, j * 128:(j + 1) * 128], pb.rearrange("p (k t) -> p k t", k=8), AF.Copy,
                reads=[bank_res[b]], writes=[R_hT[j]])

        def vp_job(g, l, j):
            tiles, sts = GROUPS[g]
            nt = len(tiles)
            t = tiles[j]
            W = wsched[(g, l)]
            is_s = (t == 16)
            b = alloc_banks(2)

            def vjob(pe):
                for (c0, n, bo) in ((0, 512, 0), (512, 256, 1)):
                    for k in range(8):
                        ins = pe.matmul(ps[:, 512 * (b + bo):512 * (b + bo) + n],
                                        lhsT=hT[:, k, j * 128:(j + 1) * 128],
                                        rhs=W["v"].ap[:, k, c0:c0 + n], start=(k == 0), stop=(k == 7))
                return ins
            emit(PE, vjob, reads=[R_hT[j]] + W["v"].res, writes=[bank_res[b], bank_res[b + 1]])
            vps = ps[:, 512 * b:512 * b + WA]
            si = alloc_stat()
            ju = alloc_scr()
            act(scr[:, ju, :].bitcast(BF16)[:, 0:WA], vps, AF.Square, reads=[bank_res[b], bank_res[b + 1]],
                writes=[R_scr[ju], R_stat[si]], scale=float(WA) ** -0.5, accum=stats[:, si:si + 1])
            rs = rstd_from(si)
            if is_s:
                dve_stt(stg_v[:], vps, rs, vg_bc[:], ALU.mult, ALU.mult,
                        reads=[bank_res[b], bank_res[b + 1], R_stat[si], R["vg_bc"]],
                        writes=[R_stg_v])
                emit(SP, lambda q: q.dma_start(out=vs_o[l], in_=stg_v[:]), reads=[R_stg_v], dsem=d_vs)
                dve_copy(vn[:, j, :], stg_v[:], reads=[R_stg_v], writes=[R_vn[j]])
            else:
                dve_stt(vn[:, j, :], vps, rs, vg_bc[:], ALU.mult, ALU.mult,
                        reads=[bank_res[b], bank_res[b + 1], R_stat[si], R["vg_bc"]], writes=[R_vn[j]])
            b = alloc_banks()

            def pjob(pe):
                for k in range(8):
                    ins = pe.matmul(bank(b), lhsT=hT[:, k, j * 128:(j + 1) * 128],
                                    rhs=W["p"].ap[:, k, :], start=(k == 0), stop=(k == 7))
                return ins
            emit(PE, pjob, reads=[R_hT[j]] + W["p"].res, writes=[bank_res[b]])
            act(ptm[:, j + 1, :], bank(b), AF.Copy, reads=[bank_res[b]], writes=[R_ptm[j + 1]])
            if t == 15:
                act(stg_p[64:128, :], ps[64:128, 512 * b:512 * b + 512], AF.Copy,
                    reads=[bank_res[b]], writes=[R_stg_p])
                emit(SP, lambda q: q.dma_start(out=nsp_p[l], in_=stg_p[113:128, :]),
                     reads=[R_stg_p], dsem=d_nsp)
                dve_copy(phalo[:, l, :], ptm[:, j + 1, :], reads=[R_ptm[j + 1]], writes=[R_phalo[l]])
            elif is_s:
                act(stg_p[:], bank(b), AF.Copy, reads=[bank_res[b]], writes=[R_stg_p])
                for s in range(SB):
                    emit(SP, lambda q, s=s: q.dma_start(out=nsp_s[l, s, 7:15, :],
                                                        in_=stg_p[ST * s:ST * s + ST, :]),
                         reads=[R_stg_p], dsem=d_nsp)
            elif j == nt - 1:
                dve_copy(phalo[:, l, :], ptm[:, j + 1, :], reads=[R_ptm[j + 1]], writes=[R_phalo[l]])

        def out_job(g, l, j):
            tiles, sts = GROUPS[g]
            t = tiles[j]
            si_ = 0 if j < sts[1][0] else 1
            b = alloc_banks(2)

            def ojob(pe):
                for half in range(2):
                    for k in range(16):
                        ins = pe.matmul(bank(b + half), lhsT=mix[:, k, j * 128:(j + 1) * 128],
                                        rhs=wout[:, k, half * 512:(half + 1) * 512],
                                        start=(k == 0), stop=(k == 15))
                return ins
            emit(PE, ojob, reads=[R_mix[k][si_] for k in range(16)] + R_wout + alias_res,
                 writes=[bank_res[b], bank_res[b + 1]])
            ops_ = ps[:, 512 * b:512 * b + D]
            si = alloc_stat()
            ju = alloc_scr()
            act(scr[:, ju, :].bitcast(BF16), ops_, AF.Square, reads=[bank_res[b], bank_res[b + 1]],
                writes=[R_scr[ju], R_stat[si]], scale=float(D) ** -0.5, accum=stats[:, si:si + 1])
            rs = rstd_from(si)
            tm = alloc_scr(2)
            tmap = scr[:, tm:tm + 2, :].rearrange("p a b -> p (a b)")
            dve_stt(tmap, ops_, rs, post_bc[:], ALU.mult, ALU.mult,
                    reads=[bank_res[b], bank_res[b + 1], R_stat[si], R["post_bc"]],
                    writes=[R_scr[tm], R_scr[tm + 1]])
            dve_tt(xg[:, j, :], xg[:, j, :], tmap, ALU.add,
                   reads=[R_x[j], R_scr[tm], R_scr[tm + 1]], writes=[R_x[j]])
            if l == DEPTH - 1:
                dst = y_s if t == 16 else y_p[t * 128:(t + 1) * 128, :]
                emit(SP, lambda q: q.dma_start(out=dst, in_=xg[:, j, :]), reads=[R_x[j]], dsem=d_x[j])

        def conv_state_out(l):
            b = alloc_banks()

            def trs(pe):
                pe.transpose(out=ps[:, 512 * b:512 * b + 128], in_=ccfull[:], identity=c32[:, C32_ID, :])
                pe.transpose(out=ps[:, 512 * b + 128:512 * b + 256],
                             in_=scarry[:, l * 6:l * 6 + 4, :].rearrange("p a b -> p (a b)"),
                             identity=c32[:, C32_ID, :])
                return pe.transpose(out=ps[:, 512 * b + 256:512 * b + 384],
                                    in_=scarry[:, l * 6 + 2:l * 6 + 6, :].rearrange("p a b -> p (a b)"),
                                    identity=c32[:, C32_ID, :])
            emit(PE, trs, reads=[R["ccarry"], R["scarry"], R["c32"]], writes=[bank_res[b]])
            act(stg_c[:], bank(b, 384), AF.Copy, reads=[bank_res[b]], writes=[R_stg_c])
            for c in range(6):
                r0 = l * 12 + 2 * c
                emit(SP, lambda q, r0=r0, c=c: q.dma_start(out=nsc_p[l][:, c * 128:(c + 1) * 128],
                                                          in_=stg_c[r0:r0 + 2, 0:128]),
                     reads=[R_stg_c], dsem=d_nsc)
                if c < 4:
                    src = stg_c[32 * c:32 * c + 32, 128:256]
                else:
                    src = stg_c[32 * (c - 2):32 * (c - 2) + 32, 256:384]
                emit(SP, lambda q, src=src, c=c: q.dma_start(out=nsc_s[l][:, c * 128:(c + 1) * 128], in_=src),
                     reads=[R_stg_c], dsem=d_nsc)

        def run_pass(g, l, prev_pass, next_pass):
            tiles, sts = GROUPS[g]
            nt = len(tiles)
            W = wsched[(g, l)]

            npv = len(GROUPS[prev_pass[0]][0]) if prev_pass is not None else 0
            same_group = prev_pass is not None and prev_pass[0] == g
            L = 0 if (same_group or prev_pass is None) else 2
            state = {"pro": False, "vp": 0}
            wout_hi = list(range(8, 16))
            t_em = set()

            def vp_next():
                if not state["pro"]:
                    state["pro"] = True
                    need(W["v"])
                    need(W["p"])
                    if g > 0:
                        dve_copy(ptm[:, 0, :], phalo[:, l, :], reads=[R_phalo[l]], writes=[R_ptm[0]])
                vp_job(g, l, state["vp"])
                state["vp"] += 1
                for _ in range(2):
                    if wout_hi:
                        wout_piece(l, wout_hi.pop(0))

            if prev_pass is None:
                emit(SP, lambda q: q.dma_start(out=xg[:, 0, :], in_=x_p[0:128, :]), writes=[R_x[0]], dsem=d_x[0])
                emit(SP, lambda s_: s_.dma_start(out=msel[:], in_=msel_d), writes=[R["msel"]], dsem=d_msel)
                first_bias = load_params_early(0, 0, defer=True)
            TS = 2
            t_pending = list(range(nt))

            def t_next():
                jt_ = t_pending.pop(0)
                t_job(jt_)
                t_em.add(jt_)

            for j in range(max(npv, nt + L)):
                if j < npv:
                    out_job(prev_pass[0], prev_pass[1], j)
                    if j == 0 and prev_pass[0] == len(GROUPS) - 1:
                        conv_state_out(prev_pass[1])
                if j < nt and not same_group and not (prev_pass is None and j == 0):
                    t = tiles[j]
                    src = x_s if t == 16 else x_p[t * 128:(t + 1) * 128, :]
                    ex = []
                    if prev_pass is None and j >= 3:
                        ex = [W["p"].res[0].w]
                    emit(SP, lambda q, src=src: q.dma_start(out=xg[:, j, :], in_=src),
                         writes=[R_x[j]], dsem=d_x[j], extra=ex)
                jn = j - L
                if t_pending and t_pending[0] <= jn - TS:
                    t_next()
                if 0 <= jn < nt:
                    n_ew(g, l, jn)
                if prev_pass is None and j == 0:
                    setup_pool_late()
                if prev_pass is None and j == 1:
                    first_bias()
                if j >= npv and state["vp"] in t_em and (state["vp"] + 1 in t_em or not t_pending):
                    vp_next()
            cnt = 0
            while state["vp"] < nt:
                if state["vp"] not in t_em:
                    t_next()
                    cnt = 0
                elif t_pending and cnt >= 1:
                    t_next()
                    cnt = 0
                else:
                    vp_next()
                    cnt += 1
            release(W["v"])
            release(W["p"])
            load_post(l)
            if prev_pass is None:
                setup_loads()
                setup_wmt()
            if g == 2:
                prep_sample_layer(l)

            for gi in range(4):
                wz = need(W[("zb", gi)])
                dbanks = []
                for si_, (j0, n, is_s) in enumerate(sts):
                    b = alloc_banks()
                    dbanks.append(b)
                    N = n * 128

                    def djob(pe, b=b, j0=j0, n=n, is_s=is_s):
                        for jj in range(n):
                            j = j0 + jj
                            o = ps[:, 512 * b + jj * 128:512 * b + (jj + 1) * 128]
                            lhs = ptm[:, j + 1, gi * 128:(gi + 1) * 128]
                            if tiles[j] == 16:
                                pe.matmul(o, lhsT=lhs, rhs=cstb[:, CB_MCURS + gi, :], start=True, stop=False)
                                pe.matmul(o, lhsT=histp[:, 0, gi * 128:(gi + 1) * 128],
                                          rhs=cstb[0:120, CB_MHIST + gi, :], start=False, stop=False)
                                ins = pe.matmul(o, lhsT=histp[:, 1, gi * 128:(gi + 1) * 128],
                                                rhs=cstb[0:120, CB_MHIST + 4 + gi, :], start=False, stop=True)
                            elif tiles[j] == 0:
                                ins = pe.matmul(o, lhsT=lhs, rhs=cstb[:, CB_MFIRST + gi, :], start=True, stop=True)
                            else:
                                pe.matmul(o, lhsT=lhs, rhs=cstb[:, CB_MCUR + gi, :], start=True, stop=False)
                                ins = pe.matmul(o, lhsT=ptm[:, j, gi * 128:(gi + 1) * 128],
                                                rhs=cstb[:, CB_MPREV + gi, :], start=False, stop=True)
                        return ins
                    rds = [R_ptm[j0 + jj + 1] for jj in range(n)] + [R_ptm[j0], R["cstb"]]
                    if is_s:
                        rds.append(R["histp"])
                    emit(PE, djob, reads=rds, writes=[bank_res[b]])
                dsb = []
                for si_, (j0, n, is_s) in enumerate(sts):
                    N = n * 128
                    d = alloc_scr()
                    dsb.append(d)
                    act(scr[:, d, :].bitcast(BF16)[:, 0:N], bank(dbanks[si_], N), AF.Copy,
                        reads=[bank_res[dbanks[si_]]], writes=[R_scr[d]])
                gzs = []
                for si_, (j0, n, is_s) in enumerate(sts):
                    N = n * 128
                    b = alloc_banks()

                    def fjob(pe, b=b, j0=j0, N=N, wz=wz):
                        for k in range(8):
                            ins = pe.matmul(bank(b, N), lhsT=wz.ap[:, k, :], rhs=hT[:, k, j0 * 128:j0 * 128 + N],
                                            start=(k == 0), stop=(k == 7))
                        return ins
                    emit(PE, fjob, reads=[R_hT[j0 + jj] for jj in range(n)] + wz.res, writes=[bank_res[b]])
                    gz = alloc_scr()
                    gzs.append(gz)
                    act(scr[:, gz, 0:N], bank(b, N), AF.Silu, reads=[bank_res[b]], writes=[R_scr[gz]])
                release(wz)
                for si_, (j0, n, is_s) in enumerate(sts):
                    N = n * 128
                    b = alloc_banks()
                    d = dsb[si_]
                    emit(PE, lambda pe, b=b, N=N, d=d: pe.matmul(
                        bank(b, N), lhsT=wpg[:, l * 4 + gi, :], rhs=scr[:, d, :].bitcast(BF16)[:, 0:N],
                        start=True, stop=True),
                        reads=[R_scr[d], R["wpg"]], writes=[bank_res[b]])
                    gz = gzs[si_]
                    dve_stt(mix[:, 6 + gi, j0 * 128:j0 * 128 + N], bank(b, N),
                            pscale[:, l * 4 + gi:l * 4 + gi + 1], scr[:, gz, 0:N], ALU.mult, ALU.mult,
                            reads=[bank_res[b], R_scr[gz], R["pscale"]], writes=[R_mix[6 + gi][si_]])

            for h in range(6):
                wza = need(W[("za", h)])
                wu = need(W[("u", h)])
                t1s = []
                for si_, (j0, n, is_s) in enumerate(sts):
                    N = n * 128
                    b = alloc_banks()

                    def fz(pe, b=b, j0=j0, N=N):
                        for k in range(8):
                            ins = pe.matmul(bank(b, N), lhsT=wza.ap[:, k, :], rhs=hT[:, k, j0 * 128:j0 * 128 + N],
                                            start=(k == 0), stop=(k == 7))
                        return ins
                    hres = [R_hT[j0 + jj] for jj in range(n)]
                    emit(PE, fz, reads=hres + wza.res, writes=[bank_res[b]])
                    ga = alloc_scr()
                    act(scr[:, ga, 0:N], bank(b, N), AF.Silu, reads=[bank_res[b]], writes=[R_scr[ga]])
                    b = alloc_banks()

                    def fu(pe, b=b, j0=j0, N=N):
                        for k in range(8):
                            ins = pe.matmul(bank(b, N), lhsT=wu.ap[:, k, :], rhs=hT[:, k, j0 * 128:j0 * 128 + N],
                                            start=(k == 0), stop=(k == 7))
                        return ins
                    emit(PE, fu, reads=hres + wu.res, writes=[bank_res[b]])
                    t1 = alloc_scr()
                    t1s.append(t1)
                    dve_tt(scr[:, t1, 0:N], bank(b, N), scr[:, ga, 0:N], ALU.mult,
                           reads=[bank_res[b], R_scr[ga]], writes=[R_scr[t1]])
                release(wza)
                release(wu)
                for si_, (j0, n, is_s) in enumerate(sts):
                    N = n * 128
                    b = alloc_banks()

                    def sjob(pe, b=b, j0=j0, n=n, is_s=is_s):
                        npr = n - 1 if is_s else n
                        if npr > 0:
                            brhs = bias2[:, h * 128:(h + 1) * 128].unsqueeze(1).to_broadcast([128, npr, 128])
                            pe.matmul(ps[:, 512 * b:512 * b + npr * 128].rearrange("p (n t) -> p n t", n=npr),
                                      lhsT=cstb[:, CB_ONES, :], rhs=brhs, start=True, stop=False)
                            for jj in range(npr):
                                j = j0 + jj
                                o = ps[:, 512 * b + jj * 128:512 * b + (jj + 1) * 128]
                                ins = pe.matmul(o, lhsT=vn[:, j, h * 128:(h + 1) * 128], rhs=wmt[:, l * 6 + h, :],
                                                start=False, stop=(jj == npr - 1))
                        if is_s:
                            j = j0 + n - 1
                            o = ps[:, 512 * b + (n - 1) * 128:512 * b + n * 128]
                            pe.matmul(o, lhsT=cstb[:, CB_ONES, :], rhs=bias2[:, 768 + h * 128:768 + (h + 1) * 128],
                                      start=True, stop=False)
                            ins = pe.matmul(o, lhsT=vn[:, j, h * 128:(h + 1) * 128], rhs=bdt[:, h, :],
                                            start=False, stop=True)
                        return ins
                    emit(PE, sjob, reads=[R_vn[j0 + jj] for jj in range(n)] + [R["wmt"], R["bdt"], R["bias2"], R["cstb"]],
                         writes=[bank_res[b]])
                    t1 = t1s[si_]
                    dve_tt(mix[:, h, j0 * 128:j0 * 128 + N], bank(b, N), scr[:, t1, 0:N], ALU.mult,
                           reads=[bank_res[b], R_scr[t1]], writes=[R_mix[h][si_]])

            wex = toks_of(alias_res)
            if next_pass is not None:
                load_params_early(*next_pass)

            for c in range(6):
                wzc = need(W[("zc", c)])
                wcg = need(W[("cg", c)])
                wxc = need(W[("xc", c)])
                wbg = need(W[("bg", c)])
                lc = l * 6 + c
                for si_, (j0, n, is_s) in enumerate(sts):
                    N = n * 128
                    hres = [R_hT[j0 + jj] for jj in range(n)]

                    def fjob_of(wa, b):
                        def f(pe, b=b, j0=j0, N=N, wa=wa):
                            for k in range(8):
                                ins = pe.matmul(bank(b, N), lhsT=wa.ap[:, k, :], rhs=hT[:, k, j0 * 128:j0 * 128 + N],
                                                start=(k == 0), stop=(k == 7))
                            return ins
                        return f
                    b = alloc_banks()
                    emit(PE, fjob_of(wzc, b), reads=hres + wzc.res, writes=[bank_res[b]])
                    gc = alloc_scr()
                    act(scr[:, gc, 0:N], bank(b, N), AF.Silu, reads=[bank_res[b]], writes=[R_scr[gc]])
                    b = alloc_banks()
                    emit(PE, fjob_of(wcg, b), reads=hres + wcg.res, writes=[bank_res[b]])
                    cgs = alloc_scr()
                    act(scr[:, cgs, 0:N], bank(b, N), AF.Copy, reads=[bank_res[b]], writes=[R_scr[cgs]])
                    b = alloc_banks()
                    emit(PE, fjob_of(wxc, b), reads=hres + wxc.res, writes=[bank_res[b]])
                    cb_ = (c * 2 + si_) % 2
                    npr = n - 1 if is_s else n
                    Np = npr * 128
                    y = alloc_scr()
                    cw = convw[:, lc * 3:lc * 3 + 3]
                    segs = []
                    late = []
                    if npr > 0:
                        act(cxb[:, cb_, 0:2], ccarry[:, lc, :], AF.Copy, reads=[R["ccarry"]], writes=[R_cxb[cb_]])
                        dve_tt(cxb[:, cb_, 2:2 + Np], bank(b, Np), scr[:, cgs, 0:Np], ALU.mult,
                               reads=[bank_res[b], R_scr[cgs]], writes=[R_cxb[cb_]])
                        late.append(lambda: act(ccarry[:, lc, :], cxb[:, cb_, Np:Np + 2], AF.Copy,
                                                reads=[R_cxb[cb_]], writes=[R["ccarry"]]))
                        segs.append(([cxb[:, cb_, 0:Np], cxb[:, cb_, 1:1 + Np], cxb[:, cb_, 2:2 + Np]],
                                     scr[:, y, 0:Np]))
                    if is_s:
                        o_s = (2 + Np) if npr > 0 else 0

                        def v3(ap):
                            return ap.rearrange("p (s k) -> p s k", k=ST)
                        cx3 = cxb[:, cb_, o_s:o_s + SB * 10].rearrange("p (s k) -> p s k", k=10)
                        act(cx3[:, :, 0:2], histc[:, c, :].rearrange("p (s k) -> p s k", k=2), AF.Copy,
                            reads=[R["histc"]], writes=[R_cxb[cb_]])
                        dve_tt(cx3[:, :, 2:10], v3(ps[:, 512 * b + Np:512 * b + Np + 128]),
                               v3(scr[:, cgs, Np:Np + 128]), ALU.mult,
                               reads=[bank_res[b], R_scr[cgs]], writes=[R_cxb[cb_]])
                        late.append(lambda: act(scarry[:, lc, :].rearrange("p (s k) -> p s k", k=2), cx3[:, :, 8:10],
                                                AF.Copy, reads=[R_cxb[cb_]], writes=[R["scarry"]]))
                        segs.append(([cx3[:, :, 0:8], cx3[:, :, 1:9], cx3[:, :, 2:10]],
                                     v3(scr[:, y, Np:Np + 128])))
                    for (taps, yv) in segs:
                        emit(ACT, lambda a_, yv=yv, taps=taps: a_.activation(out=yv, in_=taps[0], func=AF.Copy, scale=cw[:, 0:1]),
                             reads=[R_cxb[cb_], R["convw"]], writes=[R_scr[y]])
                        dve_stt(yv, taps[1], cw[:, 1:2], yv, ALU.mult, ALU.add,
                                reads=[R_cxb[cb_], R["convw"], R_scr[y]], writes=[R_scr[y]])
                        dve_stt(yv, taps[2], cw[:, 2:3], yv, ALU.mult, ALU.add,
                                reads=[R_cxb[cb_], R["convw"], R_scr[y]], writes=[R_scr[y]])
                    for f_ in late:
                        f_()
                    b = alloc_banks()
                    emit(PE, fjob_of(wbg, b), reads=hres + wbg.res, writes=[bank_res[b]])
                    dve_tt(scr[:, gc, 0:N], bank(b, N), scr[:, gc, 0:N], ALU.mult,
                           reads=[bank_res[b], R_scr[gc]], writes=[R_scr[gc]])
                    dve_tt(mix[:, 10 + c, j0 * 128:j0 * 128 + N], scr[:, y, 0:N], scr[:, gc, 0:N], ALU.mult,
                           reads=[R_scr[y], R_scr[gc]], writes=[R_mix[10 + c][si_]])
                release(wcg)
                release(wxc)
                release(wzc)
                release(wbg)
                if c < 4:
                    wout_piece(l, 2 * c, wex)
                    wout_piece(l, 2 * c + 1, wex)

        setup_pool()
        prev = None
        for pi, (g, l) in enumerate(pass_list):
            nxt = pass_list[pi + 1] if pi + 1 < len(pass_list) else None
            run_pass(g, l, prev, nxt)
            prev = (g, l)
        load_post_done = True
        for j in range(len(GROUPS[prev[0]][0])):
            out_job(prev[0], prev[1], j)
            if j == 0:
                conv_state_out(prev[1])

        for dsm in d_x + [d_out, d_stage, d_vs, d_nsp, d_nsc]:
            if dsm.count:
                nc.sync.wait_ge(dsm.sem, dsm.count)
    return nc


_CACHE = {}


def kernel(x_prompt, x_sample, state_pool, state_conv, pre_norm_g, w_in, v_norm_g,
           w_spatial, b_spatial, w_pool_group, pool_scale, conv_w, w_out, post_norm_g):
    f = lambda a: np.ascontiguousarray(np.asarray(a, dtype=np.float32))
    x_prompt, x_sample, state_pool, state_conv = map(f, (x_prompt, x_sample, state_pool, state_conv))
    pre_norm_g, w_in, v_norm_g, w_spatial, b_spatial = map(f, (pre_norm_g, w_in, v_norm_g, w_spatial, b_spatial))
    w_pool_group, pool_scale, conv_w, w_out, post_norm_g = map(f, (w_pool_group, pool_scale, conv_w, w_out, post_norm_g))

    if "nc" not in _CACHE:
        _CACHE["nc"] = build_nc()
    nc = _CACHE["nc"]

    cstb, cst32 = _host_consts()
    msel = np.zeros((128, 2), np.float32)
    msel[0, 0] = 1.0
    msel[1, 1] = 1.0
    pre_b = np.ascontiguousarray(np.broadcast_to(pre_norm_g[:, None, :], (DEPTH, 128, D)))
    post_b = np.ascontiguousarray(np.broadcast_to(post_norm_g[:, None, :], (DEPTH, 128, D)))
    vg_b = np.ascontiguousarray(np.broadcast_to(v_norm_g[:, None, :], (DEPTH, 128, WA)))
    b_prompt = b_spatial.reshape(DEPTH, 768)
    b_samp = np.tile(b_spatial[:, :, :ST], (1, 1, 128 // ST)).reshape(DEPTH, 768)
    b_sp = np.ascontiguousarray(np.stack([b_prompt, b_samp], axis=1))
    pscale = np.ascontiguousarray(pool_scale.reshape(DEPTH, 4, 128).transpose(2, 0, 1).reshape(128, DEPTH * 4))
    convw = np.ascontiguousarray(
        conv_w.reshape(DEPTH, 3, 6, 128).transpose(3, 0, 2, 1).reshape(128, DEPTH * 6 * 3))

    in_maps = []
    for b in range(NCORE):
        in_maps.append({
            "x_p": x_prompt[b],
            "x_s": np.ascontiguousarray(x_sample[SB * b:SB * (b + 1)].reshape(128, D)),
            "sp_in": np.ascontiguousarray(state_pool[:, SB * b:SB * (b + 1)].reshape(DEPTH, SB * 15, WB)),
            "sc_in": np.ascontiguousarray(state_conv[:, SB * b:SB * (b + 1)].reshape(DEPTH, SB * 2, WC)),
            "pre_g": pre_b, "post_g": post_b, "v_g": vg_b,
            "w_in": w_in, "w_sp": w_spatial, "b_sp": b_sp, "w_pg": w_pool_group,
            "pscale": pscale, "convw": convw, "w_out": w_out,
            "cstb": cstb, "cst32": cst32, "msel": msel,
        })
    res = run_bass_kernel_spmd(nc, in_maps, core_ids=list(range(NCORE)))
    rs = res.results
    y_p = np.stack([rs[b]["y_p"] for b in range(NCORE)], axis=0)
    y_s = np.concatenate([rs[b]["y_s"].reshape(SB, ST, D) for b in range(NCORE)], axis=0)
    nsp_p = np.stack([rs[b]["nsp_p"] for b in range(NCORE)], axis=1)
    nsc_p = np.stack([rs[b]["nsc_p"] for b in range(NCORE)], axis=1)
    nsp_s = np.concatenate([rs[b]["nsp_s"] for b in range(NCORE)], axis=1)
    nsc_s = np.concatenate([rs[b]["nsc_s"].reshape(DEPTH, SB, 2, WC) for b in range(NCORE)], axis=1)
    vs = np.concatenate([rs[b]["vs_o"].reshape(DEPTH, SB, ST, WA) for b in range(NCORE)], axis=1)
    return (y_p.astype(np.float32), y_s.astype(np.float32), nsp_p.astype(np.float32),
            nsc_p.astype(np.float32), nsp_s.astype(np.float32), nsc_s.astype(np.float32),
            vs.astype(np.float32))
```
